# Optimizing a Trainium2 kernel written in Bass

```python
import jax, jax.numpy as jnp
from jax import lax
import numpy as np

D_MODEL = 1024
BATCH = 2
SEQ = 16384
DEPTH = 1
DEC_BATCH = 32
DEC_SEQ = 32
PAST_LEN = 1024

CHUNK = 64
D_SSM = 1024
SSM_HEAD_DIM = 64
SSM_HEADS = D_SSM // SSM_HEAD_DIM
SSM_GROUPS = 2
SSM_HEADS_PER_GROUP = SSM_HEADS // SSM_GROUPS
SSM_STATE = 128
CONV_W = 4
CONV_DIM = D_SSM + 2 * SSM_GROUPS * SSM_STATE
D_RET = 1024
RET_HEADS = 8
RET_HEAD_DIM = D_RET // RET_HEADS
ROPE_BASE = 10000.0
D_MIX = D_SSM + D_RET
IN_PROJ_DIM = D_SSM + CONV_DIM + SSM_HEADS + 4 * D_RET
D_FF = ((8 * D_MODEL // 3 + 255) // 256) * 256
EPS = 1e-6

kernel_name = "hybrid_ssd_retention_stream_step"


def rms_norm(x, g):
    xf = x.astype(jnp.float32)
    y = xf * lax.rsqrt(jnp.mean(xf * xf, -1, keepdims=True) + EPS)
    return (y * g.astype(jnp.float32)).astype(x.dtype)


def rotary(x, pos):
    half = x.shape[-1] // 2
    inv = ROPE_BASE ** (-jnp.arange(half, dtype=jnp.float32) / half)
    ang = pos.astype(jnp.float32)[:, None] * inv[None, :]
    cos = jnp.cos(ang)[None, :, None, :]
    sin = jnp.sin(ang)[None, :, None, :]
    x1, x2 = x[..., :half], x[..., half:]
    return jnp.concatenate([x1 * cos - x2 * sin, x1 * sin + x2 * cos], -1)


def causal_conv(xbc, conv_state, w, b):
    L = xbc.shape[1]
    xp = jnp.concatenate([conv_state.astype(xbc.dtype), xbc], 1)
    y = sum(xp[:, k:k + L] * w[k] for k in range(CONV_W)) + b
    return jax.nn.silu(y), xp[:, -(CONV_W - 1):]


def ssd_scan(x, dt, a, bm, cm, h0):
    bsz, L = x.shape[:2]
    T = min(CHUNK, L)
    nc = L // T
    G, E, P, N = SSM_GROUPS, SSM_HEADS_PER_GROUP, SSM_HEAD_DIM, SSM_STATE
    x = x.reshape(bsz, nc, T, G, E, P)
    dt = dt.reshape(bsz, nc, T, G, E)
    bm = bm.reshape(bsz, nc, T, G, N)
    cm = cm.reshape(bsz, nc, T, G, N)
    a_cum = jnp.cumsum(dt * a.reshape(G, E), axis=2)
    seg = a_cum[:, :, :, None] - a_cum[:, :, None, :]
    tril = (jnp.arange(T)[:, None] >= jnp.arange(T)[None, :])[:, :, None, None]
    decay = jnp.exp(jnp.where(tril, seg, -jnp.inf))
    cb = jnp.einsum('bclgn,bcsgn->bclsg', cm, bm)
    m = cb[..., None] * decay * dt[:, :, None]
    y_diag = jnp.einsum('bclsge,bcsgep->bclgep', m, x)
    decay_s = jnp.exp(a_cum[:, :, -1:] - a_cum)
    states = jnp.einsum('bcsgn,bcsge,bcsgep->bcgepn', bm, decay_s * dt, x)
    chunk_decay = jnp.exp(a_cum[:, :, -1])

    def step(h, inp):
        s, d = inp
        return h * d[..., None, None] + s, h

    h_last, h_prev = lax.scan(step, h0.reshape(bsz, G, E, P, N),
                              (states.swapaxes(0, 1), chunk_decay.swapaxes(0, 1)))
    h_prev = h_prev.swapaxes(0, 1)
    y_off = jnp.einsum('bclgn,bcgepn,bclge->bclgep', cm, h_prev, jnp.exp(a_cum))
    y = (y_diag + y_off).reshape(bsz, L, SSM_HEADS, P)
    return y, h_last.reshape(bsz, SSM_HEADS, P, N)


def retention_scan(q, k, v, s0):
    bsz, L = q.shape[:2]
    T = min(CHUNK, L)
    nc = L // T
    H, D = RET_HEADS, RET_HEAD_DIM
    q = q.reshape(bsz, nc, T, H, D)
    k = k.reshape(bsz, nc, T, H, D)
    v = v.reshape(bsz, nc, T, H, D)
    lg = jnp.log(1.0 - 2.0 ** (-5.0 - jnp.arange(H, dtype=jnp.float32)))
    idx = jnp.arange(T, dtype=jnp.float32)
    rel = idx[:, None] - idx[None, :]
    causal = (rel >= 0)[..., None]
    dmask = jnp.where(causal, jnp.exp(jnp.where(causal, rel[..., None], 0.0) * lg), 0.0)
    scores = jnp.einsum('bclhd,bcshd->bchls', q, k) * dmask.transpose(2, 0, 1)
    y_in = jnp.einsum('bchls,bcshe->bclhe', scores, v)
    q_decay = jnp.exp((idx + 1.0)[:, None] * lg)
    k_decay = jnp.exp((T - 1.0 - idx)[:, None] * lg)
    chunk_states = jnp.einsum('bcshd,sh,bcshe->bchde', k, k_decay, v)
    chunk_decay = jnp.exp(T * lg)[:, None, None]

    def step(s, cs):
        return s * chunk_decay + cs, s

    s_last, s_prev = lax.scan(step, s0, chunk_states.swapaxes(0, 1))
    s_prev = s_prev.swapaxes(0, 1)
    y_cross = jnp.einsum('bclhd,lh,bchde->bclhe', q, q_decay, s_prev)
    return (y_in + y_cross).reshape(bsz, L, H, D), s_last


def hybrid_layer(x, pos, conv_state, ssm_state, ret_state,
                 n_mix_pre, n_mix_post, n_ffn_pre, n_ffn_post, w_in, conv_w, conv_b,
                 dt_bias, a_log, d_skip, ssm_norm, ret_norm, w_out, w_gate, w_up, w_down):
    bsz, L, _ = x.shape
    h = rms_norm(x, n_mix_pre)
    proj = h @ w_in
    o1 = D_SSM; o2 = o1 + CONV_DIM; o3 = o2 + SSM_HEADS
    o4 = o3 + D_RET; o5 = o4 + D_RET; o6 = o5 + D_RET
    z, xbc, dt_raw, q, k, v, g = jnp.split(proj, [o1, o2, o3, o4, o5, o6], -1)

    xbc, new_conv = causal_conv(xbc, conv_state, conv_w, conv_b)
    xbc = xbc.astype(jnp.float32)
    xs = xbc[..., :D_SSM].reshape(bsz, L, SSM_HEADS, SSM_HEAD_DIM)
    bm = xbc[..., D_SSM:D_SSM + SSM_GROUPS * SSM_STATE].reshape(bsz, L, SSM_GROUPS, SSM_STATE)
    cm = xbc[..., D_SSM + SSM_GROUPS * SSM_STATE:].reshape(bsz, L, SSM_GROUPS, SSM_STATE)
    dt = jax.nn.softplus(dt_raw.astype(jnp.float32) + dt_bias.astype(jnp.float32))
    a = -jnp.exp(a_log.astype(jnp.float32))
    y_ssm, new_ssm = ssd_scan(xs, dt, a, bm, cm, ssm_state.astype(jnp.float32))
    y_ssm = (y_ssm + d_skip.astype(jnp.float32)[:, None] * xs).reshape(bsz, L, D_SSM)
    y_ssm = y_ssm * jax.nn.silu(z.astype(jnp.float32))
    yg = y_ssm.reshape(bsz, L, SSM_GROUPS, D_SSM // SSM_GROUPS)
    yg = yg * lax.rsqrt(jnp.mean(yg * yg, -1, keepdims=True) + EPS)
    y_ssm = yg.reshape(bsz, L, D_SSM) * ssm_norm.astype(jnp.float32)

    qh = rotary(q.astype(jnp.float32).reshape(bsz, L, RET_HEADS, RET_HEAD_DIM), pos)
    kh = rotary(k.astype(jnp.float32).reshape(bsz, L, RET_HEADS, RET_HEAD_DIM), pos) * RET_HEAD_DIM ** -0.5
    vh = v.astype(jnp.float32).reshape(bsz, L, RET_HEADS, RET_HEAD_DIM)
    y_ret, new_ret = retention_scan(qh, kh, vh, ret_state.astype(jnp.float32))
    mu = jnp.mean(y_ret, -1, keepdims=True)
    yc = y_ret - mu
    y_ret = yc * lax.rsqrt(jnp.mean(yc * yc, -1, keepdims=True) + EPS)
    y_ret = y_ret.reshape(bsz, L, D_RET) * ret_norm.astype(jnp.float32) * jax.nn.silu(g.astype(jnp.float32))

    mix = jnp.concatenate([y_ssm, y_ret], -1).astype(x.dtype) @ w_out
    x = x + rms_norm(mix, n_mix_post)

    hf = rms_norm(x, n_ffn_pre)
    ff = (jax.nn.silu(hf @ w_gate) * (hf @ w_up)) @ w_down
    x = x + rms_norm(ff, n_ffn_post)
    return x, new_conv.astype(x.dtype), new_ssm.astype(x.dtype), new_ret.astype(x.dtype)


def setup_inputs(seed: int = 0) -> dict:
    key = jax.random.key(seed)
    ks = jax.random.split(key, 24)
    f32 = jnp.float32
    nrm = lambda k, s, sc: jax.random.normal(k, s, f32) * sc
    dt0 = jnp.exp(jax.random.uniform(ks[9], (DEPTH, SSM_HEADS), f32) * (np.log(0.1) - np.log(0.001)) + np.log(0.001))
    return {
        "x_prompt": nrm(ks[0], (BATCH, SEQ, D_MODEL), 1.0),
        "x_sample": nrm(ks[1], (DEC_BATCH, DEC_SEQ, D_MODEL), 1.0),
        "state_conv": nrm(ks[2], (DEPTH, DEC_BATCH, CONV_W - 1, CONV_DIM), 1.0),
        "state_ssm": nrm(ks[3], (DEPTH, DEC_BATCH, SSM_HEADS, SSM_HEAD_DIM, SSM_STATE), 0.1),
        "state_ret": nrm(ks[4], (DEPTH, DEC_BATCH, RET_HEADS, RET_HEAD_DIM, RET_HEAD_DIM), 0.5),
        "n_mix_pre": 1.0 + nrm(ks[5], (DEPTH, D_MODEL), 0.05),
        "n_mix_post": 1.0 + nrm(ks[6], (DEPTH, D_MODEL), 0.05),
        "n_ffn_pre": 1.0 + nrm(ks[7], (DEPTH, D_MODEL), 0.05),
        "n_ffn_post": 1.0 + nrm(ks[8], (DEPTH, D_MODEL), 0.05),
        "w_in": nrm(ks[10], (DEPTH, D_MODEL, IN_PROJ_DIM), D_MODEL ** -0.5),
        "conv_w": nrm(ks[11], (DEPTH, CONV_W, CONV_DIM), CONV_W ** -0.5),
        "conv_b": nrm(ks[12], (DEPTH, CONV_DIM), 0.02),
        "dt_bias": dt0 + jnp.log(-jnp.expm1(-dt0)),
        "a_log": jnp.log(jax.random.uniform(ks[13], (DEPTH, SSM_HEADS), f32, 1.0, 16.0)),
        "d_skip": 1.0 + nrm(ks[14], (DEPTH, SSM_HEADS), 0.1),
        "ssm_norm": 1.0 + nrm(ks[15], (DEPTH, D_SSM), 0.05),
        "ret_norm": 1.0 + nrm(ks[16], (DEPTH, D_RET), 0.05),
        "w_out": nrm(ks[17], (DEPTH, D_MIX, D_MODEL), D_MIX ** -0.5),
        "w_gate": nrm(ks[18], (DEPTH, D_MODEL, D_FF), D_MODEL ** -0.5),
        "w_up": nrm(ks[19], (DEPTH, D_MODEL, D_FF), D_MODEL ** -0.5),
        "w_down": nrm(ks[20], (DEPTH, D_FF, D_MODEL), D_FF ** -0.5),
    }


def reference(x_prompt, x_sample, state_conv, state_ssm, state_ret,
              n_mix_pre, n_mix_post, n_ffn_pre, n_ffn_post, w_in, conv_w, conv_b,
              dt_bias, a_log, d_skip, ssm_norm, ret_norm, w_out, w_gate, w_up, w_down):
    bp, lp = x_prompt.shape[:2]
    ls = x_sample.shape[1]
    pos_p = jnp.arange(lp)
    pos_s = PAST_LEN + jnp.arange(ls)
    yp, ys = x_prompt, x_sample
    conv_p, ssm_p, ret_p, conv_s, ssm_s, ret_s = [], [], [], [], [], []
    for l in range(DEPTH):
        w = (n_mix_pre[l], n_mix_post[l], n_ffn_pre[l], n_ffn_post[l], w_in[l], conv_w[l], conv_b[l],
             dt_bias[l], a_log[l], d_skip[l], ssm_norm[l], ret_norm[l], w_out[l], w_gate[l], w_up[l], w_down[l])
        zc = jnp.zeros((bp, CONV_W - 1, CONV_DIM), yp.dtype)
        zs = jnp.zeros((bp, SSM_HEADS, SSM_HEAD_DIM, SSM_STATE), jnp.float32)
        zr = jnp.zeros((bp, RET_HEADS, RET_HEAD_DIM, RET_HEAD_DIM), jnp.float32)
        yp, c1, s1, r1 = hybrid_layer(yp, pos_p, zc, zs, zr, *w)
        ys, c2, s2, r2 = hybrid_layer(ys, pos_s, state_conv[l], state_ssm[l], state_ret[l], *w)
        conv_p.append(c1); ssm_p.append(s1); ret_p.append(r1)
        conv_s.append(c2); ssm_s.append(s2); ret_s.append(r2)
    return (yp, ys, jnp.stack(conv_p), jnp.stack(ssm_p), jnp.stack(ret_p),
            jnp.stack(conv_s), jnp.stack(ssm_s), jnp.stack(ret_s))
```

```python
import numpy as np
import ml_dtypes
from contextlib import ExitStack
import concourse.bass as bass
import concourse.mybir as mybir
from concourse.bass_utils import run_bass_kernel_spmd

F32, BF16 = mybir.dt.float32, mybir.dt.bfloat16
AF, ALU, AX = mybir.ActivationFunctionType, mybir.AluOpType, mybir.AxisListType

D = 1024
NH, HP, NG, NS = 16, 64, 2, 128
CONV = 1536
RH, RD = 8, 128
DFF = 2816
INP = 6672
EPS = 1e-6
PAST_LEN = 1024
O_Z, O_XBC, O_DT, O_Q, O_K, O_V, O_G = 0, 1024, 2560, 2576, 3600, 4624, 5648
NEG = -30000.0
NCORES = 8
TA_XS, TA_B, TA_K, TA_V, TA_W = 0, 1024, 1280, 2304, 3328
TF_Z, TF_G, TF_DT, TF_DTA, TF_W = 0, 1024, 2048, 2064, 2080
FM_B, FM_C, FM_Q, FM_K, FM_N = 0, 2, 4, 12, 20
XW = 2 * 1024 + 16


class Sem:
    def __init__(self, h, step):
        self.h = h
        self.step = step
        self.n = 0


class Buf:
    __slots__ = ("name", "w", "r", "excl")

    def __init__(self, name="", excl=False):
        self.name = name
        self.w = None
        self.r = {}
        self.excl = excl


class Eng:
    def __init__(self, name, sem):
        self.name = name
        self.sem = sem
        self.known = {}
        self.dma_sems = []
        self.dma_rr = 0
        self.prog = []
        self.nwait = 0
        self.nins = 0

    def need(self, sem, cnt):
        if cnt <= 0 or self.known.get(id(sem), 0) >= cnt:
            return
        h, v = sem.h, cnt * sem.step
        self.prog.append(lambda e, h=h, v=v: e.wait_ge(h, v))
        self.known[id(sem)] = cnt
        self.nwait += 1


class Trk:
    def __init__(self):
        self.engs = {}
        self.all_sems = []

    def add_engine(self, name, sem_h, dma_sem_hs=()):
        en = Eng(name, Sem(sem_h, 1))
        en.dma_sems = [Sem(h, 16) for h in dma_sem_hs]
        self.engs[name] = en
        self.all_sems.append(en.sem)
        self.all_sems.extend(en.dma_sems)
        return en

    def _deps(self, eng, reads, writes, is_dma):
        for b in reads:
            if b.w is not None:
                eng.need(b.w[0], b.w[1])
        for b in writes:
            if b.w is not None and (is_dma or b.w[2] is not eng):
                eng.need(b.w[0], b.w[1])
            for k, (sem, cnt) in b.r.items():
                if is_dma or k is not eng:
                    eng.need(sem, cnt)

    def op(self, eng, fn, reads=(), writes=()):
        ex = [b for b in reads if b.excl]
        if ex:
            writes = list(writes) + [b for b in ex if b not in writes]
        self._deps(eng, reads, writes, False)
        eng.sem.n += 1
        eng.nins += 1
        h = eng.sem.h
        eng.prog.append(lambda e, fn=fn, h=h: fn(e).then_inc(h, 1))
        rec = (eng.sem, eng.sem.n, eng)
        for b in writes:
            b.w = rec
            b.r = {}
        for b in reads:
            b.r[eng] = (eng.sem, eng.sem.n)

    def ext(self, q, fn, sem, reads=(), writes=()):
        q.need(sem, sem.n)
        self._deps(q, reads, writes, True)
        h, st = sem.h, sem.step
        q.prog.append(lambda e, fn=fn, h=h, st=st: fn(e).then_inc(h, st))
        sem.n += 1
        rec = (sem, sem.n, None)
        for b in writes:
            b.w = rec
            b.r = {}
        for b in reads:
            b.r[("x", id(sem))] = (sem, sem.n)

    def dma(self, q, out, in_, reads=(), writes=(), **kw):
        s = q.dma_sems[q.dma_rr % len(q.dma_sems)]
        q.dma_rr += 1
        self.ext(q, lambda e, out=out, in_=in_, kw=kw: e.dma_start(out=out, in_=in_, **kw), s, reads, writes)

    def wait_all(self, eng, bufs):
        for b in bufs:
            if b.w is not None:
                eng.need(b.w[0], b.w[1])
            for k, (sem, cnt) in b.r.items():
                eng.need(sem, cnt)

    def barrier(self):
        for en in self.engs.values():
            for s in self.all_sems:
                if s is not en.sem:
                    en.need(s, s.n)

    def emit(self, nc):
        names = {"pe": "tensor", "dve": "vector", "act": "scalar", "pool": "gpsimd", "sp": "sync"}
        with nc.Block() as block:
            for nm, en in self.engs.items():
                def body(e, en=en):
                    for f in en.prog:
                        f(e)
                getattr(block, names[nm])(body)


class Arena:
    def __init__(self, ap, nwords):
        self.ap = ap
        self.n = nwords
        self.off = 0
        self.peak = 0

    def mark(self):
        return self.off

    def reset(self, m):
        self.off = m

    def alloc(self, shape, dt):
        elems = int(np.prod(shape[1:]))
        nbytes = elems * (4 if dt == F32 else 2)
        words = ((nbytes + 31) // 32) * 8
        assert self.off + words <= self.n, f"arena overflow: need {self.off + words} of {self.n}"
        v = self.ap[:, self.off:self.off + words]
        self.off += words
        self.peak = max(self.peak, self.off)
        if dt != F32:
            v = v.bitcast(dt)
        v = v[:, 0:elems]
        if len(shape) == 3:
            v = v.rearrange("p (a b) -> p a b", a=shape[1], b=shape[2])
        elif len(shape) == 4:
            v = v.rearrange("p (a b c) -> p a b c", a=shape[1], b=shape[2], c=shape[3])
        return v


def bcast(ap, axis, n):
    v = ap.unsqueeze(axis)
    shp = list(v.shape)
    shp[axis] = n
    return v.broadcast_to(shp)


def const_layout():
    items = [("gpre_mix", 8), ("gpre_ffn", 8), ("wo_norm", 16), ("cw", 48), ("cb", 12), ("dtb", 16), ("alog", 16),
             ("dskip", 16), ("gpost_mix", 1024), ("gpost_ffn", 1024), ("ident", 128), ("dret", 8), ("smask", 8)]
    for ty in ("p", "s"):
        items += [(f"triU_{ty}", 128), (f"onesblk_{ty}", 128), (f"negm_{ty}", 128), (f"m01_{ty}", 128), (f"dq_{ty}", 8), (f"dk_{ty}", 8),
                  (f"gT_{ty}", 8), (f"selrow_{ty}", 4), (f"selblk_{ty}", 512)]
    lay = {}
    off = 0
    for k, w in items:
        lay[k] = (off, w)
        off += w
    return lay, off


def host_consts(core, LP, params):
    lay, ncst = const_layout()
    C = np.zeros((128, ncst), np.float32)

    def put(k, a):
        o, w = lay[k]
        C[:, o:o + w] = np.asarray(a, np.float32).reshape(128, w)

    put("gpre_mix", params["n_mix_pre"].reshape(8, 128).T)
    put("gpre_ffn", params["n_ffn_pre"].reshape(8, 128).T)
    put("wo_norm", np.concatenate([params["ssm_norm"], params["ret_norm"]]).reshape(16, 128).T)
    cw = params["conv_w"].reshape(4, 12, 128)
    put("cw", cw.transpose(2, 1, 0).reshape(128, 48))
    put("cb", params["conv_b"].reshape(12, 128).T)
    put("dtb", np.broadcast_to(params["dt_bias"], (128, 16)))
    put("alog", np.broadcast_to(params["a_log"], (128, 16)))
    put("dskip", np.broadcast_to(params["d_skip"], (128, 16)))
    put("gpost_mix", np.broadcast_to(params["n_mix_post"], (128, 1024)))
    put("gpost_ffn", np.broadcast_to(params["n_ffn_post"], (128, 1024)))
    put("ident", np.eye(128, dtype=np.float32))
    lg = np.log(1.0 - 2.0 ** (-5.0 - np.arange(RH, dtype=np.float64)))
    put("dret", np.broadcast_to(np.exp(LP * lg), (128, 8)))
    seg = core % 4
    sm = np.zeros(8)
    base = (core // 4) * 4
    sm[base:base + seg] = 1.0
    put("smask", np.broadcast_to(sm, (128, 8)))
    for ty, bl in (("p", 128), ("s", 32)):
        idx = np.arange(128)
        blk = idx // bl
        l = idx % bl
        same = blk[:, None] == blk[None, :]
        put(f"triU_{ty}", (same & (idx[:, None] <= idx[None, :])).astype(np.float32))
        put(f"onesblk_{ty}", same.astype(np.float32))
        causal = same & (idx[None, :] >= idx[:, None])
        put(f"negm_{ty}", np.where(causal, 0.0, NEG))
        put(f"m01_{ty}", causal.astype(np.float32))
        put(f"dq_{ty}", np.exp((l[:, None] + 1.0) * lg[None, :]))
        put(f"dk_{ty}", np.exp(-(l[:, None] + 1.0) * lg[None, :]) * RD ** -0.5)
        put(f"gT_{ty}", np.broadcast_to(np.exp(bl * lg), (128, 8)))
        nb = 128 // bl
        sr = np.zeros((128, 4))
        sb = np.zeros((128, 4, 128))
        for i in range(4):
            if i < nb:
                sr[:, i] = (blk == i)
                sb[:, i, :] = (blk == i)[:, None]
        put(f"selrow_{ty}", sr)
        put(f"selblk_{ty}", sb.reshape(128, 512))
    return C


def rope_table(pos):
    half = RD // 2
    inv = (10000.0 ** (-np.arange(half, dtype=np.float32) / half)).astype(np.float32)
    ang = pos.astype(np.float32)[:, None] * inv[None, :]
    return np.concatenate([np.cos(ang), np.sin(ang)], 1).astype(np.float32)


class Prog:
    def __init__(self, LP, debug=False, stop=9):
        assert LP % 256 == 0
        self.LP = LP
        self.NPT = LP // 128
        self.NT = self.NPT + 1
        self.debug = debug
        self.lay, self.ncst = const_layout()
        self.nc = nc = bass.Bass("TRN2", target_bir_lowering=False)
        NT = self.NT

        def din(name, shape, dt=F32):
            return nc.dram_tensor(name, list(shape), dt, kind="ExternalInput").ap()

        def dout(name, shape, dt=F32):
            return nc.dram_tensor(name, list(shape), dt, kind="ExternalOutput").ap()

        def dint(name, shape, dt=F32):
            if debug:
                return nc.dram_tensor(name, list(shape), dt, kind="ExternalOutput").ap()
            return nc.dram_tensor(name, list(shape), dt).ap()

        self.xin = din("xin", [NT * 128, D])
        self.xhalo = din("xhalo", [128, D])
        self.cs = din("cs", [NT * 128, 128])
        self.convst = din("convst", [128, 144])
        self.sst = din("sst", [128, 4 * 1024])
        self.rst = din("rst", [128, 4 * 1024])
        self.w_in = din("w_in", [D, INP])
        self.w_out = din("w_out", [2 * D, D])
        self.w_gate = din("w_gate", [D, DFF])
        self.w_up = din("w_up", [D, DFF])
        self.w_down = din("w_down", [DFF, D])
        self.cpk = din("cpk", [128, self.ncst])
        self.y = dout("y", [NT * 128, D])
        self.oconv_p = dout("oconv_p", [128, 36])
        self.oconv_s = dout("oconv_s", [128, 144])
        self.ossm_p = dout("ossm_p", [128, 1024])
        self.oret_p = dout("oret_p", [128, 1024])
        self.ossm_s = dout("ossm_s", [128, 4 * 1024])
        self.oret_s = dout("oret_s", [128, 4 * 1024])
        self.tokA = dint("tokA", [NT * 128, TA_W], BF16)
        self.tokF = dint("tokF", [NT * 128, TF_W])
        self.fmA = dint("fmA", [NT * 128, FM_N * 128], BF16)
        self.x1 = dint("x1", [NT * 128, D])
        self.send = nc.dram_tensor("send", [128, XW], F32)
        self.recv = nc.dram_tensor("recv", [NCORES * 128, XW], F32)

        self.out_bufs = []
        with ExitStack() as es:
            self.es = es
            AW = 53000
            arena_t = es.enter_context(nc.sbuf_tensor("arena", [128, AW], F32))
            self.A = Arena(arena_t[:], AW)
            self.ps = es.enter_context(nc.psum_tensor("ps", [128, 8 * 512], F32))
            self.psB = [Buf(f"ps{i}", excl=True) for i in range(8)]
            self.ps_rr = 0
            self.ring_rr = {"A": 0, "B": 0}
            T = self.T = Trk()

            def sem(name):
                return es.enter_context(nc.semaphore(name))

            self.pe = T.add_engine("pe", sem("s_pe"))
            self.dve = T.add_engine("dve", sem("s_dve"))
            self.act = T.add_engine("act", sem("s_act"))
            self.pool = T.add_engine("pool", sem("s_pool"), [sem(f"dq{i}") for i in range(8)])
            self.sp = T.add_engine("sp", sem("s_sp"), [sem(f"ds{i}") for i in range(16)])
            self.cc_sem = Sem(sem("s_cc"), 1)
            T.all_sems.append(self.cc_sem)

            self.scr = {}
            self.phase0()
            if stop >= 1:
                self.phase1()
            if stop >= 2:
                self.exchange()
            if stop >= 3:
                self.phase2()
            if stop >= 4:
                self.phase3()
            T.wait_all(self.sp, self.out_bufs)
            T.barrier()
            T.emit(nc)

    def cst(self, k):
        o, w = self.lay[k]
        return self.consts[:, o:o + w]

    def bank(self, i=None, ring=None):
        if ring is not None:
            base = 0 if ring == "A" else 4
            i = base + self.ring_rr[ring] % 4
            self.ring_rr[ring] += 1
        elif i is None:
            i = self.ps_rr % 8
            self.ps_rr += 1
        return self.ps[:, i * 512:(i + 1) * 512], [self.psB[i]]

    def bank2(self, i=None):
        if i is None:
            if self.ps_rr % 2:
                self.ps_rr += 1
            i = self.ps_rr % 8
            self.ps_rr += 2
        return self.ps[:, i * 512:(i + 2) * 512], [self.psB[i], self.psB[i + 1]]

    def mm(self, out, lhsT, rhs, start, stop, reads, writes):
        self.T.op(self.pe, lambda e: e.matmul(out, lhsT=lhsT, rhs=rhs, start=start, stop=stop), reads, writes)

    def tr(self, out, in_, reads, writes):
        idb = self.identb
        self.T.op(self.pe, lambda e: e.transpose(out, in_, idb), list(reads) + [self.B_const], writes)

    def tt(self, eng, out, a, b, op, reads, writes):
        self.T.op(eng, lambda e: e.tensor_tensor(out, a, b, op), reads, writes)

    def ts(self, eng, out, a, s1, s2, op0, op1, reads, writes):
        if s2 is None:
            self.T.op(eng, lambda e: e.tensor_scalar(out, a, s1, None, op0=op0), reads, writes)
        else:
            self.T.op(eng, lambda e: e.tensor_scalar(out, a, s1, s2, op0=op0, op1=op1), reads, writes)

    def stt(self, out, a, s, b, op0, op1, reads, writes):
        self.T.op(self.dve, lambda e: e.scalar_tensor_tensor(out, a, s, b, op0=op0, op1=op1), reads, writes)

    def actf(self, out, a, func, reads, writes, scale=1.0, bias=0.0, accum=None):
        if accum is None:
            self.T.op(self.act, lambda e: e.activation(out, a, func, bias=bias, scale=scale), reads, writes)
        else:
            self.T.op(self.act, lambda e: e.activation(out, a, func, bias=bias, scale=scale, accum_out=accum), reads, writes)

    def cp(self, eng, out, a, reads, writes):
        if eng is self.act:
            self.T.op(eng, lambda e: e.copy(out, a), reads, writes)
        else:
            self.T.op(eng, lambda e: e.tensor_copy(out, a), reads, writes)

    def red(self, out, a, reads, writes):
        self.T.op(self.dve, lambda e: e.tensor_reduce(out, a, axis=AX.X, op=ALU.add), reads, writes)

    def rstd_from_ss(self, rstd, ss, tmp, n, reads_b, write_b):
        self.actf(tmp, ss, AF.Ln, reads_b, write_b, scale=1.0 / n, bias=EPS)
        self.actf(rstd, tmp, AF.Exp, write_b, write_b, scale=-0.5)

    def phase0(self):
        T, A = self.T, self.A
        self.consts = A.alloc([128, self.ncst], F32)
        self.B_const = Buf("const")
        T.dma(self.sp, self.consts, self.cpk[:, :], writes=[self.B_const])
        self.identb = A.alloc([128, 128], BF16)
        self.cp(self.dve, self.identb, self.cst("ident"), [self.B_const], [self.B_const])
        self.a_bc = A.alloc([128, 16], F32)
        self.actf(self.a_bc, self.cst("alog"), AF.Exp, [self.B_const], [self.B_const])
        self.ts(self.dve, self.a_bc, self.a_bc, -1.0, None, ALU.mult, None, [self.B_const], [self.B_const])
        self.m_base = A.mark()

    def load_weight(self, dst, src_rows, ncols, gscale, kchunks, piece):
        T, A = self.T, self.A
        T.barrier()
        m = A.mark()
        npc = ncols // piece
        assert npc * piece == ncols
        stage = [A.alloc([128, piece], F32) for _ in range(4)]
        sB = [Buf() for _ in range(4)]
        wB = Buf()
        i = 0
        for k in range(kchunks):
            for pc in range(npc):
                s = i % 4
                q = self.sp if i % 2 == 0 else self.pool
                T.dma(q, stage[s], src_rows[k * 128:(k + 1) * 128, pc * piece:(pc + 1) * piece], writes=[sB[s]])
                o = dst[:, k, pc * piece:(pc + 1) * piece]
                if i % 2 == 0:
                    if gscale is None:
                        self.cp(self.dve, o, stage[s], [sB[s]], [wB])
                    else:
                        self.ts(self.dve, o, stage[s], gscale[:, k:k + 1], None, ALU.mult, None, [sB[s], self.B_const], [wB])
                else:
                    if gscale is None:
                        self.cp(self.act, o, stage[s], [sB[s]], [wB])
                    else:
                        self.actf(o, stage[s], AF.Copy, [sB[s], self.B_const], [wB], scale=gscale[:, k:k + 1])
                i += 1
        A.reset(m)
        return wB

    def phase1(self):
        T, A = self.T, self.A
        pe, dve, act, pool, sp = self.pe, self.dve, self.act, self.pool, self.sp
        NPT, NT = self.NPT, self.NT
        A.reset(self.m_base)
        winb = A.alloc([128, 8, INP], BF16)
        B_w = self.load_weight(winb, self.w_in, INP, self.cst("gpre_mix"), 8, 1668)
        T.barrier()
        S = A.alloc([128, 16, 64], F32)
        Sr = A.alloc([128, 8, 128], F32)
        Dc = A.alloc([128, 16], F32)
        B_S, B_Sr, B_Dc = Buf(), Buf(), Buf()
        T.op(pool, lambda e: e.memset(S, 0.0), [], [B_S])
        T.op(pool, lambda e: e.memset(Sr, 0.0), [], [B_Sr])
        T.op(pool, lambda e: e.memset(Dc, 1.0), [], [B_Dc])
        xt = [A.alloc([128, D], F32) for _ in range(2)]
        cst_ = [A.alloc([128, 128], F32) for _ in range(2)]
        hb = [A.alloc([128, D], BF16) for _ in range(2)]
        hT = [A.alloc([128, 8, 128], BF16) for _ in range(2)]
        tokA_s = [A.alloc([128, TA_W], BF16) for _ in range(2)]
        tokF_s = [A.alloc([128, TF_W], F32)] * 2
        fmA_s = [A.alloc([128, FM_N, 128], BF16)] * 2
        q_tok = A.alloc([128, 8, 128], BF16)
        bcf = [A.alloc([128, 4, 128], BF16) for _ in range(2)]
        B_bcf = [Buf() for _ in range(2)]
        rtab = [A.alloc([128, 8, 64], F32) for _ in range(4)]
        rtmp = [[A.alloc([128, 4, 64], F32) for _ in range(4)]] * 2
        raw = A.alloc([128, 12 * 140], F32)
        acc = A.alloc([128, 12, 128], F32)
        xs_fm = A.alloc([128, 8, 128], BF16)
        xw = A.alloc([128, 16, 64], BF16)
        st = [A.alloc([128, 8], F32) for _ in range(2)]
        sm16 = A.alloc([128, 8, 16], F32)
        print("phase1 arena peak words", A.peak, "of", A.n)
        B_xt = [Buf() for _ in range(2)]
        B_cs = [Buf() for _ in range(2)]
        B_hb = [Buf() for _ in range(2)]
        B_hT = [Buf() for _ in range(2)]
        B_st = [Buf() for _ in range(2)]
        B_tA = [{k: Buf() for k in ("xsB", "k", "v")} for _ in range(2)]
        B_tF = [{k: Buf() for k in ("z", "g", "dt")}] * 2
        B_fm = [{k: Buf() for k in ("bc", "q", "k")}] * 2
        B_qt, B_rtab = Buf(), Buf()
        B_rtmp = [Buf()] * 2
        B_raw, B_xsfm, B_xw, B_sm = Buf(), Buf(), Buf(), Buf()
        B_acc = [Buf() for _ in range(12)]
        rr_grp = [0]

        def stageA(t):
            halo = t < 0
            sample = (t == NPT)
            par = t & 1
            ty = "s" if sample else "p"
            NB, BL = (4, 32) if sample else (1, 128)
            tc = self.tile_consts(ty)
            src = self.xhalo[:, :] if halo else self.xin[t * 128:(t + 1) * 128, :]
            T.dma(sp, xt[par], src, writes=[B_xt[par]])
            if not halo:
                T.dma(sp, cst_[par], self.cs[t * 128:(t + 1) * 128, :], writes=[B_cs[par]])
            ss, rs, rstd = st[par][:, 0:1], st[par][:, 1:2], st[par][:, 2:3]
            self.actf(hb[par], xt[par], AF.Square, [B_xt[par]], [B_hb[par], B_st[par]], accum=ss)
            self.rstd_from_ss(rstd, ss, rs, D, [B_st[par]], [B_st[par]])
            self.ts(dve, hb[par], xt[par], rstd, None, ALU.mult, None, [B_xt[par], B_st[par]], [B_hb[par]])
            pb, pB = self.bank(ring="A")
            pbb = pb.bitcast(BF16)
            for k in range(8):
                self.tr(pbb[:, k * 128:(k + 1) * 128], hb[par][:, k * 128:(k + 1) * 128], [B_hb[par]], pB)
            self.cp(act, hT[par].rearrange("p a b -> p (a b)"), pbb, pB, [B_hT[par]])
            yield
            if sample:
                rawv = raw.rearrange("p (j s c) -> p j s c", j=12, s=4, c=35)
                T.dma(sp, rawv[:, :, :, 0:3], self.convst.rearrange("p (j s c) -> p j s c", j=12, s=4, c=3), writes=[B_raw])
            else:
                rawv = raw[:, 0:12 * 131].rearrange("p (j c) -> p j c", j=12, c=131)
            for jj in range(3):
                pb, pB = self.bank(ring="A")
                for j4 in range(4):
                    j = jj * 4 + j4
                    for k in range(8):
                        self.mm(pb[:, j4 * 128:(j4 + 1) * 128], winb[:, k, O_XBC + j * 128:O_XBC + (j + 1) * 128],
                                hT[par][:, k, :], k == 0, k == 7, [B_hT[par], B_w], pB)
                pv = pb.rearrange("p (a b) -> p a b", a=4, b=128)
                if halo:
                    self.cp(dve, rawv[:, jj * 4:jj * 4 + 4, 0:3], pv[:, :, 125:128], pB, [B_raw])
                elif sample:
                    self.cp(dve, rawv[:, jj * 4:jj * 4 + 4, :, 3:35], pb.rearrange("p (a s c) -> p a s c", a=4, s=4, c=32), pB, [B_raw])
                else:
                    self.cp(dve, rawv[:, jj * 4:jj * 4 + 4, 3:131], pv, pB, [B_raw])
                yield
            yield
            if halo:
                return
            cw, cb = self.cst("cw"), self.cst("cb")
            for k in range(4):
                for j in range(12):
                    if sample:
                        src_ = rawv[:, j, :, k:k + 32]
                        dst_ = acc[:, j, :].rearrange("p (s c) -> p s c", s=4, c=32)
                    else:
                        src_ = rawv[:, j, k:k + 128]
                        dst_ = acc[:, j, :]
                    wk = cw[:, j * 4 + k:j * 4 + k + 1]
                    if k == 0:
                        self.ts(dve, dst_, src_, wk, cb[:, j:j + 1], ALU.mult, ALU.add, [B_raw, self.B_const], [B_acc[j]])
                    else:
                        self.stt(dst_, src_, wk, dst_, ALU.mult, ALU.add, [B_raw, self.B_const, B_acc[j]], [B_acc[j]])
                    if j % 3 == 2:
                        yield
            self.actf(xs_fm.rearrange("p a b -> p (a b)"), acc[:, 0:8, :].rearrange("p a b -> p (a b)"), AF.Silu,
                      B_acc[0:8], [B_xsfm])
            self.actf(bcf[par].rearrange("p a b -> p (a b)"), acc[:, 8:12, :].rearrange("p a b -> p (a b)"),
                      AF.Silu, B_acc[8:12], [B_bcf[par]])
            yield
            if sample:
                T.dma(sp, self.oconv_s.rearrange("p (j s c) -> p j s c", j=12, s=4, c=3), rawv[:, :, :, 32:35], reads=[B_raw], writes=[self._ob()])
            else:
                self.cp(pool, rawv[:, :, 0:3], rawv[:, :, 128:131], [B_raw], [B_raw])
                if t == NPT - 1:
                    T.dma(sp, self.oconv_p.rearrange("p (j c) -> p j c", j=12, c=3), rawv[:, :, 0:3], reads=[B_raw], writes=[self._ob()])
            pb, pB = self.bank(ring="A")
            pbb = pb.bitcast(BF16)
            for j in range(8):
                self.tr(pbb[:, j * 128:(j + 1) * 128], xs_fm[:, j, :], [B_xsfm], pB)
            self.cp(dve, tokA_s[par][:, TA_XS:TA_XS + 1024], pbb, pB, [B_tA[par]["xsB"]])
            yield
            pb, pB = self.bank(ring="A")
            pbb = pb.bitcast(BF16)
            for g in range(2):
                self.tr(pbb[:, g * 128:(g + 1) * 128], bcf[par][:, g, :], [B_bcf[par]], pB)
            self.cp(dve, tokA_s[par][:, TA_B:TA_B + 256], pbb[:, 0:256], pB, [B_tA[par]["xsB"]])
            yield

        def stageB(t):
            sample = (t == NPT)
            par = t & 1
            ty = "s" if sample else "p"
            tc = self.tile_consts(ty)
            cosv, sinv = cst_[par][:, 0:64], cst_[par][:, 64:128]
            for i, (tb, dec) in enumerate(((cosv, "dq"), (sinv, "dq"), (cosv, "dk"), (sinv, "dk"))):
                self.tt(pool, rtab[i], bcast(tb, 1, 8), bcast(tc[dec], 2, 64), ALU.mult, [B_cs[par], self.B_const, B_rtab], [B_rtab])
            def group(c0, n):
                pb, pB = self.bank(ring="B")
                for k in range(8):
                    self.mm(pb[:, 0:n], hT[par][:, k, :], winb[:, k, c0:c0 + n], k == 0, k == 7, [B_hT[par], B_w], pB)
                return pb, pB

            for half in range(2):
                pb, pB = group(O_Z + half * 512, 512)
                self.actf(tokF_s[par][:, TF_Z + half * 512:TF_Z + (half + 1) * 512], pb, AF.Silu, pB, [B_tF[par]["z"]])
                yield
            for half in range(2):
                pb, pB = group(O_G + half * 512, 512)
                self.actf(tokF_s[par][:, TF_G + half * 512:TF_G + (half + 1) * 512], pb, AF.Silu, pB, [B_tF[par]["g"]])
                yield
            for half in range(2):
                pb, pB = group(O_V + half * 512, 512)
                self.cp(dve, tokA_s[par][:, TA_V + half * 512:TA_V + (half + 1) * 512], pb, pB, [B_tA[par]["v"]])
                yield
            pb, pB = group(O_DT, 16)
            dts = tokF_s[par][:, TF_DT:TF_DT + 16]
            self.tt(dve, sm16[:, 0, :], pb[:, 0:16], self.cst("dtb"), ALU.add, pB + [self.B_const], [B_sm])
            self.actf(sm16[:, 1, :], sm16[:, 0, :], AF.Exp, [B_sm], [B_sm])
            self.actf(dts, sm16[:, 1, :], AF.Ln, [B_sm], [B_tF[par]["dt"]], bias=1.0)
            self.tt(dve, tokF_s[par][:, TF_DTA:TF_DTA + 16], dts, self.a_bc, ALU.mult, [B_tF[par]["dt"], self.B_const], [B_tF[par]["dt"]])
            for which in ("q", "k"):
                c0 = O_Q if which == "q" else O_K
                if which == "q":
                    dstv, dB = q_tok, B_qt
                    ct, stb = rtab[0], rtab[1]
                else:
                    dstv = tokA_s[par][:, TA_K:TA_K + 1024].rearrange("p (h d) -> p h d", h=8, d=128)
                    dB = B_tA[par]["k"]
                    ct, stb = rtab[2], rtab[3]
                for half in range(2):
                    pb, pB = group(c0 + half * 512, 512)
                    X = pb.rearrange("p (h d) -> p h d", h=4, d=128)
                    x1, x2 = X[:, :, 0:64], X[:, :, 64:128]
                    h0 = half * 4
                    tm = rtmp[rr_grp[0] % 2]
                    tB = B_rtmp[rr_grp[0] % 2]
                    rr_grp[0] += 1
                    self.tt(dve, tm[0], x1, ct[:, h0:h0 + 4, :], ALU.mult, pB + [B_rtab], [tB])
                    self.tt(dve, tm[1], x2, stb[:, h0:h0 + 4, :], ALU.mult, pB + [B_rtab], [tB])
                    self.tt(dve, tm[2], x1, stb[:, h0:h0 + 4, :], ALU.mult, pB + [B_rtab], [tB])
                    self.tt(dve, tm[3], x2, ct[:, h0:h0 + 4, :], ALU.mult, pB + [B_rtab], [tB])
                    self.tt(pool, dstv[:, h0:h0 + 4, 0:64], tm[0], tm[1], ALU.subtract, [tB], [dB])
                    self.tt(pool, dstv[:, h0:h0 + 4, 64:128], tm[2], tm[3], ALU.add, [tB], [dB])
                    yield
            for which in ("q", "k"):
                pb, pB = self.bank(ring="B")
                pbb = pb.bitcast(BF16)
                for h in range(8):
                    if which == "q":
                        self.tr(pbb[:, h * 128:(h + 1) * 128], q_tok[:, h, :], [B_qt], pB)
                    else:
                        self.tr(pbb[:, h * 128:(h + 1) * 128], tokA_s[par][:, TA_K + h * 128:TA_K + (h + 1) * 128], [B_tA[par]["k"]], pB)
                o = FM_Q if which == "q" else FM_K
                self.cp(act, fmA_s[par][:, o:o + 8, :].rearrange("p a b -> p (a b)"), pbb, pB, [B_fm[par][which]])
                yield
            if not sample:
                self.ssm_state_step(tokF_s[par], tokA_s[par], [B_tF[par]["dt"]], [B_tA[par]["xsB"]], tc,
                                    [(S, B_S, None)], sm16, B_sm, xw, B_xw, Dc, B_Dc, bk=(4, 6))
                yield
                self.ret_state_step(tokA_s[par], [B_tA[par]["k"]], [B_tA[par]["v"]], tc, [(Sr, B_Sr, None)], None, None, bk=4)
                yield
            rows = slice(t * 128, (t + 1) * 128)
            T.dma(sp, self.tokA[rows, :], tokA_s[par], reads=list(B_tA[par].values()), writes=[self._sb(("tokA", t))])
            T.dma(sp, self.tokF[rows, :], tokF_s[par], reads=list(B_tF[par].values()), writes=[self._sb(("tokF", t))])
            self.cp(pool, fmA_s[par][:, FM_B:FM_B + 4, :].rearrange("p a b -> p (a b)"), bcf[par].rearrange("p a b -> p (a b)"),
                    [B_bcf[par]], [B_fm[par]["bc"]])
            T.dma(sp, self.fmA[rows, :], fmA_s[par].rearrange("p a b -> p (a b)"), reads=list(B_fm[par].values()), writes=[self._sb(("fmA", t))])
            yield

        def interleave(gens):
            gens = [g for g in gens if g is not None]
            while gens:
                for g in list(gens):
                    try:
                        next(g)
                    except StopIteration:
                        gens.remove(g)

        def sp_fence():
            for s_ in T.all_sems:
                if s_ is not sp.sem:
                    sp.need(s_, s_.n)

        interleave([stageA(-1)])
        for step in range(NT + 1):
            sp_fence()
            interleave([stageA(step) if step < NT else None, stageB(step - 1) if step >= 1 else None])
        sendv = self.send.ap()
        self.B_send = Buf()
        T.dma(sp, sendv[:, 0:1024], S.rearrange("p a b -> p (a b)"), reads=[B_S], writes=[self.B_send])
        T.dma(sp, sendv[:, 1024:2048], Sr.rearrange("p a b -> p (a b)"), reads=[B_Sr], writes=[self.B_send])
        T.dma(sp, sendv[:, 2048:2064], Dc, reads=[B_Dc], writes=[self.B_send])
        T.barrier()

    def _ob(self):
        b = Buf()
        self.out_bufs.append(b)
        return b

    def _sb(self, key):
        b = self.scr.get(key)
        if b is None:
            b = self.scr[key] = Buf()
        return b

    def ssm_small(self, tokF_t, rB, tc, sm16, B_sm, bk=None):
        T, dve = self.T, self.dve
        dta = tokF_t[:, TF_DTA:TF_DTA + 16]
        dt = tokF_t[:, TF_DT:TF_DT + 16]
        pb, pB = self.bank(None if bk is None else bk[0])
        self.mm(pb[:, 0:16], tc["triU"], dta, True, True, rB + [self.B_const], pB)
        self.mm(pb[:, 16:32], tc["onesblk"], dta, True, True, rB + [self.B_const], pB)
        self.cp(dve, sm16[:, 2:4, :].rearrange("p a b -> p (a b)"), pb[:, 0:32], pB, [B_sm])
        self.actf(sm16[:, 4, :], sm16[:, 2, :], AF.Exp, [B_sm], [B_sm])
        self.tt(dve, sm16[:, 6, :], sm16[:, 3, :], sm16[:, 2, :], ALU.subtract, [B_sm], [B_sm])
        self.actf(sm16[:, 6, :], sm16[:, 6, :], AF.Exp, [B_sm], [B_sm])
        self.tt(dve, sm16[:, 5, :], sm16[:, 6, :], dt, ALU.mult, [B_sm] + rB, [B_sm])

    def tile_consts(self, ty):
        return {k: self.cst(f"{k}_{ty}") for k in ("triU", "onesblk", "negm", "m01", "dq", "dk", "gT", "selrow", "selblk")}

    def ssm_state_step(self, tokF_t, tokA_t, rBF, rBA, tc, states, sm16, B_sm, xw, B_xw, Dc=None, B_Dc=None, small_done=False, bk=None):
        T, dve, act, pool = self.T, self.dve, self.act, self.pool
        if not small_done:
            self.ssm_small(tokF_t, rBF, tc, sm16, B_sm, bk=bk)
        xs3 = tokA_t[:, TA_XS:TA_XS + 1024].rearrange("p (h d) -> p h d", h=16, d=64)
        self.tt(pool, xw, xs3, bcast(sm16[:, 5, :], 2, 64), ALU.mult, rBA + [B_sm], [B_xw])
        dta = tokF_t[:, TF_DTA:TF_DTA + 16]
        nb = len(states)
        for i, (S, B_S, sb) in enumerate(states):
            pb, pB = self.bank(None if bk is None else bk[0])
            self.mm(pb[:, 0:16], tc["selblk"][:, i * 128:(i + 1) * 128], dta, True, True, rBF + [self.B_const], pB)
            self.actf(sm16[:, 7, :], pb[:, 0:16], AF.Exp, pB, [B_sm])
            if nb > 1:
                xwi = self.xwm
                self.ts(pool, xwi.rearrange("p a b -> p (a b)"), xw.rearrange("p a b -> p (a b)"), tc["selrow"][:, i:i + 1], None,
                        ALU.mult, None, [B_xw, self.B_const], [self.B_xwm])
                xsrc, xB = xwi, self.B_xwm
            else:
                xsrc, xB = xw, B_xw
            p2, p2B = self.bank2(None if bk is None else bk[1])
            xflat = xsrc.rearrange("p a b -> p (a b)")
            for g in range(2):
                self.mm(p2[:, g * 512:(g + 1) * 512], tokA_t[:, TA_B + g * 128:TA_B + (g + 1) * 128], xflat[:, g * 512:(g + 1) * 512],
                        True, True, rBA + [xB], p2B)
            self.tt(pool, S, S, bcast(sm16[:, 7, :], 2, 64), ALU.mult, [B_S, B_sm], [B_S])
            self.tt(dve, S.rearrange("p a b -> p (a b)"), S.rearrange("p a b -> p (a b)"), p2, ALU.add, [B_S] + p2B, [B_S])
            if sb is not None:
                self.cp(act, sb[0], S.rearrange("p a b -> p (a b)"), [B_S], [sb[1]])
            if Dc is not None:
                self.tt(pool, Dc, Dc, sm16[:, 7, :], ALU.mult, [B_Dc, B_sm], [B_Dc])

    def ret_state_step(self, tokA_t, rBk, rBv, tc, states, vm, B_vm, bk=None):
        T, dve, act, pool = self.T, self.dve, self.act, self.pool
        nb = len(states)
        for i, (Sr, B_Sr, sb) in enumerate(states):
            if nb > 1:
                self.ts(pool, vm, tokA_t[:, TA_V:TA_V + 1024], tc["selrow"][:, i:i + 1], None, ALU.mult, None, rBv + [self.B_const], [B_vm])
                vsrc, vB = vm, [B_vm]
            else:
                vsrc, vB = tokA_t[:, TA_V:TA_V + 1024], rBv
            p2, p2B = self.bank2(bk)
            for h in range(8):
                self.mm(p2[:, h * 128:(h + 1) * 128], tokA_t[:, TA_K + h * 128:TA_K + (h + 1) * 128], vsrc[:, h * 128:(h + 1) * 128],
                        True, True, rBk + vB, p2B)
            Srf = Sr.rearrange("p a b -> p (a b)")
            self.tt(dve, Srf, Srf, p2, ALU.add, [B_Sr] + p2B, [B_Sr])
            self.tt(pool, Sr, Sr, bcast(tc["gT"], 2, 128), ALU.mult, [B_Sr, self.B_const], [B_Sr])
            if sb is not None:
                self.cp(act, sb[0], Srf, [B_Sr], [sb[1]])

    def exchange(self):
        T = self.T
        self.B_recv = Buf()
        send, recv = self.send, self.recv
        T.ext(self.pool, lambda e: e.collective_compute("AllGather", ALU.bypass, replica_groups=[list(range(NCORES))],
                                                        ins=[send.ap().opt()], outs=[recv.ap().opt()]),
              self.cc_sem, reads=[self.B_send], writes=[self.B_recv])

    def phase2(self):
        T, A = self.T, self.A
        pe, dve, act, pool, sp = self.pe, self.dve, self.act, self.pool, self.sp
        NPT, NT = self.NPT, self.NT
        A.reset(self.m_base)
        wob = A.alloc([128, 16, D], BF16)
        B_wo = self.load_weight(wob, self.w_out, D, self.cst("wo_norm"), 16, 1024)
        T.barrier()
        H = A.alloc([128, 16, 64], F32)
        Hr = A.alloc([128, 8, 128], F32)
        Hb = A.alloc([128, 1024], BF16)
        Hrb = A.alloc([128, 1024], BF16)
        B_H, B_Hr, B_Hb, B_Hrb = Buf(), Buf(), Buf(), Buf()
        Hf, Hrf = H.rearrange("p a b -> p (a b)"), Hr.rearrange("p a b -> p (a b)")
        T.op(pool, lambda e: e.memset(H, 0.0), [], [B_H])
        T.op(pool, lambda e: e.memset(Hr, 0.0), [], [B_Hr])
        m = A.mark()
        rbuf = [A.alloc([128, XW], F32) for _ in range(2)]
        B_rb = [Buf() for _ in range(2)]
        dm = A.alloc([128, 32], F32)
        B_dm = Buf()
        recvv = self.recv.ap()
        smask, dret = self.cst("smask"), self.cst("dret")
        for r in range(NCORES):
            rb, rB = rbuf[r % 2], B_rb[r % 2]
            T.dma(sp, rb, recvv[r * 128:(r + 1) * 128, :], reads=[self.B_recv], writes=[rB])
            smr = smask[:, r:r + 1]
            self.ts(dve, dm[:, 0:16], rb[:, 2048:2064], -1.0, smr, ALU.add, ALU.mult, [rB, self.B_const], [B_dm])
            self.ts(dve, dm[:, 0:16], dm[:, 0:16], 1.0, None, ALU.add, None, [B_dm], [B_dm])
            self.ts(dve, dm[:, 16:24], dret, -1.0, smr, ALU.add, ALU.mult, [self.B_const], [B_dm])
            self.ts(dve, dm[:, 16:24], dm[:, 16:24], 1.0, None, ALU.add, None, [B_dm], [B_dm])
            self.tt(pool, H, H, bcast(dm[:, 0:16], 2, 64), ALU.mult, [B_H, B_dm], [B_H])
            self.stt(Hf, rb[:, 0:1024], smr, Hf, ALU.mult, ALU.add, [rB, B_H, self.B_const], [B_H])
            self.tt(pool, Hr, Hr, bcast(dm[:, 16:24], 2, 128), ALU.mult, [B_Hr, B_dm], [B_Hr])
            self.stt(Hrf, rb[:, 1024:2048], smr, Hrf, ALU.mult, ALU.add, [rB, B_Hr, self.B_const], [B_Hr])
        self.cp(act, Hb, Hf, [B_H], [B_Hb])
        self.cp(act, Hrb, Hrf, [B_Hr], [B_Hrb])
        T.barrier()
        A.reset(m)
        Ss = [H] + [A.alloc([128, 16, 64], F32) for _ in range(3)]
        Srs = [Hr] + [A.alloc([128, 8, 128], F32) for _ in range(3)]
        Ssb = [Hb] + [A.alloc([128, 1024], BF16) for _ in range(3)]
        Srsb = [Hrb] + [A.alloc([128, 1024], BF16) for _ in range(3)]
        B_Ss = [B_H] + [Buf() for _ in range(3)]
        B_Srs = [B_Hr] + [Buf() for _ in range(3)]
        B_Ssb = [B_Hb] + [Buf() for _ in range(3)]
        B_Srsb = [B_Hrb] + [Buf() for _ in range(3)]
        tokA_s = [A.alloc([128, TA_W], BF16) for _ in range(2)]
        tokF_s = [A.alloc([128, TF_W], F32) for _ in range(2)]
        fmA_s = [A.alloc([128, FM_N, 128], BF16) for _ in range(2)]
        xt = [A.alloc([128, D], F32) for _ in range(2)]
        B_tA = [Buf() for _ in range(2)]
        B_tF = [Buf() for _ in range(2)]
        B_fm = [Buf() for _ in range(2)]
        B_xt = [Buf() for _ in range(2)]
        RM = A.alloc([128, 16, 128], F32)
        R, M2 = RM[:, 0:8, :], RM[:, 8:16, :]
        seg = A.alloc([128, 8, 128], F32)
        dec = A.alloc([128, 8, 128], BF16)
        mT = A.alloc([128, 16, 128], BF16)
        xdt = A.alloc([128, 16, 64], BF16)
        xw = A.alloc([128, 16, 64], BF16)
        vm = A.alloc([128, 1024], BF16)
        B_vm = Buf()
        Cm = A.alloc([128, 4, 2, 128], BF16)
        qm = RM.rearrange("p a b -> p (a b)").bitcast(BF16).rearrange("p (i h l) -> p i h l", i=4, h=8, l=128)
        ya2 = [A.alloc([128, 16, 64], F32) for _ in range(2)]
        yb = A.alloc([128, 16, 64], F32)
        mixs = [A.alloc([128, 2048], BF16) for _ in range(2)]
        scm = A.alloc([128, 8, 128], BF16)
        sq = A.alloc([128, 8, 128], F32)
        mixT = A.alloc([128, 16, 128], BF16)
        x1o = [A.alloc([128, D], F32)]
        sm16 = A.alloc([128, 8, 16], F32)
        st = A.alloc([128, 48], F32)
        print("phase2 arena peak words", A.peak, "of", A.n)
        B_R, B_M2, B_seg, B_dec, B_mT, B_xdt, B_xw = Buf(), Buf(), Buf(), Buf(), Buf(), Buf(), Buf()
        self.xwm, self.B_xwm = xdt, B_xdt
        B_Cm, B_qm, B_ya0, B_yb, B_scm, B_sq, B_mixT, B_sm, B_stA, B_stB = (Buf() for _ in range(10))
        B_ya2 = [B_ya0, Buf()]
        B_mixs = [[Buf(), Buf()] for _ in range(2)]
        B_x1o = [Buf()]
        T.op(pool, lambda e: e.memset(Cm.rearrange("p a b c -> p (a b c)"), 0.0), [], [B_Cm])
        gpost = self.cst("gpost_mix")
        dskip = self.cst("dskip")

        def common(t):
            sample = (t == NPT)
            par = t & 1
            ty = "s" if sample else "p"
            NB = 4 if sample else 1
            tc = self.tile_consts(ty)
            rows = slice(t * 128, (t + 1) * 128)
            return sample, par, NB, tc, rows

        def stageA(t):
            sample, par, NB, tc, rows = common(t)
            tA, tF, fm = tokA_s[par], tokF_s[par], fmA_s[par]
            T.dma(sp, tA, self.tokA[rows, :], reads=[self._sb(("tokA", t))], writes=[B_tA[par]])
            T.dma(sp, tF, self.tokF[rows, :], reads=[self._sb(("tokF", t))], writes=[B_tF[par]])
            T.dma(sp, fm.rearrange("p a b -> p (a b)"), self.fmA[rows, :], reads=[self._sb(("fmA", t))], writes=[B_fm[par]])
            T.dma(sp, xt[par], self.xin[rows, :], writes=[B_xt[par]])
            rA, rF, rM = [B_tA[par]], [B_tF[par]], [B_fm[par]]
            if sample:
                for i in range(4):
                    T.dma(sp, Ss[i].rearrange("p a b -> p (a b)"), self.sst[:, i * 1024:(i + 1) * 1024], writes=[B_Ss[i]])
                    self.cp(act, Ssb[i], Ss[i].rearrange("p a b -> p (a b)"), [B_Ss[i]], [B_Ssb[i]])
                sst = [(Ss[i], B_Ss[i], (Ssb[i], B_Ssb[i])) for i in range(4)]
            else:
                sst = [(H, B_H, (Hb, B_Hb))]
            yield
            dt = tF[:, TF_DT:TF_DT + 16]
            dta = tF[:, TF_DTA:TF_DTA + 16]
            self.ssm_small(tF, rF, tc, sm16, B_sm, bk=(0,))
            yield
            acum, ea = sm16[:, 2, :], sm16[:, 4, :]
            xs3 = tA[:, TA_XS:TA_XS + 1024].rearrange("p (h d) -> p h d", h=16, d=64)
            self.tt(pool, xdt, xs3, bcast(dt, 2, 64), ALU.mult, rA + rF, [B_xdt])
            pcb, pcbB = self.bank(1)
            for g in range(2):
                self.mm(pcb[:, g * 128:(g + 1) * 128], fm[:, FM_B + g, :], fm[:, FM_C + g, :], True, True, rM, pcbB)
            yield
            for g in range(2):
                hs = slice(g * 8, (g + 1) * 8)
                self.tt(pool, R, bcast(dta[:, hs], 2, 128), bcast(tc["triU"], 1, 8), ALU.mult, rF + [self.B_const, B_R], [B_R])
                yield
                self.tt(pool, M2, bcast(tc["negm"], 1, 8), bcast(acum[:, hs], 2, 128), ALU.subtract, [self.B_const, B_sm, B_M2], [B_M2])
                pa, paB = self.bank2(2)
                Rf = R.rearrange("p a b -> p (a b)")
                for c in range(2):
                    self.mm(pa[:, c * 512:(c + 1) * 512], tc["onesblk"], Rf[:, c * 512:(c + 1) * 512], True, True, [B_R, self.B_const], paB)
                yield
                self.tt(dve, seg.rearrange("p a b -> p (a b)"), pa, M2.rearrange("p a b -> p (a b)"), ALU.add, paB + [B_M2], [B_seg])
                yield
                self.actf(dec.rearrange("p a b -> p (a b)"), seg.rearrange("p a b -> p (a b)"), AF.Exp, [B_seg], [B_dec])
                yield
                self.tt(dve, mT[:, hs, :], dec, bcast(pcb[:, g * 128:(g + 1) * 128], 1, 8), ALU.mult, [B_dec] + pcbB, [B_mT])
                yield
            pyd, pydB = self.bank2(2)
            for h in range(16):
                self.mm(pyd[:, h * 64:(h + 1) * 64], mT[:, h, :], xdt[:, h, :], True, True, [B_mT, B_xdt], pydB)
                if h % 4 == 3:
                    yield
            pyo, pyoB = self.bank2(0)
            for g in range(2):
                for i in range(NB):
                    if sample:
                        self.cp(pool, Cm[:, i, g, i * 32:(i + 1) * 32], fm[:, FM_C + g, i * 32:(i + 1) * 32], rM + [B_Cm], [B_Cm])
                        lhs = Cm[:, i, g, :]
                        lB = [B_Cm]
                    else:
                        lhs = fm[:, FM_C + g, :]
                        lB = rM
                    self.mm(pyo[:, g * 512:(g + 1) * 512], lhs, sst[i][2][0][:, g * 512:(g + 1) * 512], i == 0, i == NB - 1,
                            lB + [sst[i][2][1]], pyoB)
            yield
            ya, B_ya = ya2[par], B_ya2[par]
            yaf, ybf = ya.rearrange("p a b -> p (a b)"), yb.rearrange("p a b -> p (a b)")
            self.tt(dve, ya, pyo.rearrange("p (a b) -> p a b", a=16, b=64), bcast(ea, 2, 64), ALU.mult, pyoB + [B_sm], [B_ya])
            yield
            self.tt(dve, yaf, yaf, pyd, ALU.add, [B_ya] + pydB, [B_ya])
            yield
            self.ssm_state_step(tF, tA, rF, rA, tc, sst, sm16, B_sm, xw, B_xw, small_done=True, bk=(0, 2))
            if t == NPT - 1:
                T.dma(sp, self.ossm_p[:, :], Hf, reads=[B_H], writes=[self._ob()])
            if sample:
                for i in range(4):
                    T.dma(sp, self.ossm_s[:, i * 1024:(i + 1) * 1024], Ss[i].rearrange("p a b -> p (a b)"), reads=[B_Ss[i]], writes=[self._ob()])
            yield

        def stageB(t):
            sample, par, NB, tc, rows = common(t)
            tA, tF, fm = tokA_s[par], tokF_s[par], fmA_s[par]
            rA, rF, rM = [B_tA[par]], [B_tF[par]], [B_fm[par]]
            if sample:
                for i in range(4):
                    T.dma(sp, Srs[i].rearrange("p a b -> p (a b)"), self.rst[:, i * 1024:(i + 1) * 1024], writes=[B_Srs[i]])
                    self.cp(act, Srsb[i], Srs[i].rearrange("p a b -> p (a b)"), [B_Srs[i]], [B_Srsb[i]])
                rst = [(Srs[i], B_Srs[i], (Srsb[i], B_Srsb[i])) for i in range(4)]
                T.op(pool, lambda e: e.memset(qm.rearrange("p a b c -> p (a b c)"), 0.0), [B_R, B_M2], [B_qm, B_R, B_M2])
            else:
                rst = [(Hr, B_Hr, (Hrb, B_Hrb))]
            ya, B_ya = ya2[par], B_ya2[par]
            yaf, ybf = ya.rearrange("p a b -> p (a b)"), yb.rearrange("p a b -> p (a b)")
            xs3 = tA[:, TA_XS:TA_XS + 1024].rearrange("p (h d) -> p h d", h=16, d=64)
            self.tt(pool, yb, xs3, bcast(dskip, 2, 64), ALU.mult, rA + [self.B_const], [B_yb])
            yield
            self.tt(pool, yaf, yaf, ybf, ALU.add, [B_ya, B_yb], [B_ya])
            yield
            self.tt(pool, yaf, yaf, tF[:, TF_Z:TF_Z + 1024], ALU.mult, [B_ya] + rF, [B_ya])
            yield
            for g in range(2):
                self.actf(ybf[:, g * 512:(g + 1) * 512], yaf[:, g * 512:(g + 1) * 512], AF.Square, [B_ya], [B_yb, B_stA], accum=st[:, g:g + 1])
            yield
            self.rstd_from_ss(st[:, 4:6], st[:, 0:2], st[:, 2:4], 512, [B_stA], [B_stA])
            for g in range(2):
                self.actf(mixs[par][:, g * 512:(g + 1) * 512], yaf[:, g * 512:(g + 1) * 512], AF.Copy, [B_ya, B_stA], [B_mixs[par][0]],
                          scale=st[:, 4 + g:5 + g])
            yield
            psc, pscB = self.bank2(4)
            for h in range(8):
                self.mm(psc[:, h * 128:(h + 1) * 128], fm[:, FM_K + h, :], fm[:, FM_Q + h, :], True, True, rM, pscB)
                if h % 4 == 3:
                    yield
            self.tt(dve, scm, psc.rearrange("p (a b) -> p a b", a=8, b=128), bcast(tc["m01"], 1, 8), ALU.mult, pscB + [self.B_const], [B_scm])
            yield
            pyr, pyrB = self.bank2(6)
            for h in range(8):
                self.mm(pyr[:, h * 128:(h + 1) * 128], scm[:, h, :], tA[:, TA_V + h * 128:TA_V + (h + 1) * 128], True, False, [B_scm] + rA, pyrB)
                for i in range(NB):
                    if sample:
                        self.cp(pool, qm[:, i, h, i * 32:(i + 1) * 32], fm[:, FM_Q + h, i * 32:(i + 1) * 32], rM + [B_qm], [B_qm])
                        lhs, lB = qm[:, i, h, :], [B_qm]
                    else:
                        lhs, lB = fm[:, FM_Q + h, :], rM
                    self.mm(pyr[:, h * 128:(h + 1) * 128], lhs, rst[i][2][0][:, h * 128:(h + 1) * 128], False, i == NB - 1,
                            lB + [rst[i][2][1]], pyrB)
                if h % 2 == 1:
                    yield
            pyr3 = pyr.rearrange("p (a b) -> p a b", a=8, b=128)
            s1, s2, mu, var = st[:, 8:16], st[:, 16:24], st[:, 24:32], st[:, 32:40]
            self.red(s1, pyr3, pyrB, [B_stB])
            yield
            self.actf(sq.rearrange("p a b -> p (a b)"), pyr, AF.Square, pyrB, [B_sq])
            yield
            self.red(s2, sq, [B_sq], [B_stB])
            yield
            self.ts(dve, mu, s1, 1.0 / 128, None, ALU.mult, None, [B_stB], [B_stB])
            self.tt(dve, var, mu, mu, ALU.mult, [B_stB], [B_stB])
            self.stt(var, s2, 1.0 / 128, var, ALU.mult, ALU.subtract, [B_stB], [B_stB])
            self.actf(s1, var, AF.Ln, [B_stB], [B_stB], bias=EPS)
            self.actf(s2, s1, AF.Exp, [B_stB], [B_stB], scale=-0.5)
            yield
            self.tt(dve, sq, pyr3, bcast(mu, 2, 128), ALU.subtract, pyrB + [B_stB, B_sq], [B_sq])
            yield
            self.tt(pool, sq, sq, bcast(s2, 2, 128), ALU.mult, [B_sq, B_stB], [B_sq])
            yield
            self.tt(pool, mixs[par][:, 1024:2048], sq.rearrange("p a b -> p (a b)"), tF[:, TF_G:TF_G + 1024], ALU.mult, [B_sq] + rF, [B_mixs[par][1]])
            yield
            self.ret_state_step(tA, rA, rA, tc, rst, vm, B_vm, bk=4)
            yield
            for c in range(2):
                pb, pB = self.bank(4 + c)
                pbb = pb.bitcast(BF16)
                for kk in range(8):
                    kc = c * 8 + kk
                    self.tr(pbb[:, kk * 128:(kk + 1) * 128], mixs[par][:, kc * 128:(kc + 1) * 128], [B_mixs[par][c]], pB)
                    if kk % 4 == 3:
                        yield
                self.cp(act, mixT[:, c * 8:(c + 1) * 8, :].rearrange("p a b -> p (a b)"), pbb, pB, [B_mixT])
                yield
            po, poB = self.bank2(6)
            for half in range(2):
                for kc in range(16):
                    self.mm(po[:, half * 512:(half + 1) * 512], mixT[:, kc, :], wob[:, kc, half * 512:(half + 1) * 512], kc == 0, kc == 15,
                            [B_mixT, B_wo], poB)
                    if kc % 4 == 3:
                        yield
            self.actf(sq.rearrange("p a b -> p (a b)"), po, AF.Square, poB, [B_sq, B_stB], accum=st[:, 6:7])
            yield
            self.rstd_from_ss(st[:, 7:8], st[:, 6:7], st[:, 40:41], D, [B_stB], [B_stB])
            xo = x1o[0]
            self.stt(xo, po, st[:, 7:8], gpost, ALU.mult, ALU.mult, poB + [B_stB, self.B_const], [B_x1o[0]])
            yield
            self.tt(pool, xo, xo, xt[par], ALU.add, [B_x1o[0], B_xt[par]], [B_x1o[0]])
            T.dma(sp, self.x1[rows, :], xo, reads=[B_x1o[0]], writes=[self._sb(("x1", t))])
            if t == NPT - 1:
                T.dma(sp, self.oret_p[:, :], Hrf, reads=[B_Hr], writes=[self._ob()])
            if sample:
                for i in range(4):
                    T.dma(sp, self.oret_s[:, i * 1024:(i + 1) * 1024], Srs[i].rearrange("p a b -> p (a b)"), reads=[B_Srs[i]], writes=[self._ob()])
            yield

        def interleave(gens):
            gens = [g for g in gens if g is not None]
            while gens:
                for g in list(gens):
                    try:
                        next(g)
                    except StopIteration:
                        gens.remove(g)

        for step in range(NT + 1):
            interleave([stageA(step) if step < NT else None, stageB(step - 1) if step >= 1 else None])
        T.barrier()

    def phase3(self):
        T, A = self.T, self.A
        pe, dve, act, pool, sp = self.pe, self.dve, self.act, self.pool, self.sp
        NT = self.NT
        A.reset(self.m_base)
        wg = A.alloc([128, 8, DFF], BF16)
        wu = A.alloc([128, 8, DFF], BF16)
        wd = A.alloc([128, 22, D], BF16)
        B_wg = self.load_weight(wg, self.w_gate, DFF, self.cst("gpre_ffn"), 8, 1408)
        B_wu = self.load_weight(wu, self.w_up, DFF, self.cst("gpre_ffn"), 8, 1408)
        B_wd = self.load_weight(wd, self.w_down, D, None, 22, 1024)
        T.barrier()
        NSUB = 2
        TW = NSUB * 128
        x1t = [A.alloc([128, D], F32) for _ in range(3)]
        x1r = [A.alloc([128, D], F32) for _ in range(2)]
        yo = [A.alloc([128, D], F32) for _ in range(2)]
        hb = [A.alloc([128, D], BF16) for _ in range(2)]
        hT = A.alloc([128, 8, TW], BF16)
        aT = A.alloc([128, 22, TW], BF16)
        sg = [A.alloc([128, TW], F32) for _ in range(2)]
        junk = A.alloc([128, D], BF16)
        junk2 = A.alloc([128, D], F32)
        st = [A.alloc([128, 8], F32) for _ in range(4)]
        print("phase3 arena words", A.off, "of", A.n)
        B_x1t = [Buf() for _ in range(3)]
        B_x1r = [Buf() for _ in range(2)]
        B_yo = [Buf() for _ in range(2)]
        B_hb = [Buf() for _ in range(2)]
        B_hT, B_aT, B_junk = Buf(), Buf(), Buf()
        B_sg = [Buf() for _ in range(2)]
        B_st = [Buf() for _ in range(4)]
        gpost = self.cst("gpost_ffn")
        ntiles = (NT + NSUB - 1) // NSUB
        cnt = [0, 0, 0]
        import os
        for ft in range(min(ntiles, int(os.environ.get('P3TILES', '999')))):
            subs = [s for s in range(ft * NSUB, min(NT, (ft + 1) * NSUB))]
            pb_ = os.environ.get('P3BAR', '1')
            if ft >= 1 and pb_ == '1':
                T.barrier()
            elif ft >= 1 and pb_ in T.engs:
                en = T.engs[pb_]
                for s_ in T.all_sems:
                    if s_ is not en.sem:
                        en.need(s_, s_.n)
            ns = len(subs)
            w = ns * 128
            for si, s in enumerate(subs):
                b3 = cnt[0] % 3
                cnt[0] += 1
                b2 = s % 2
                rows = slice(s * 128, (s + 1) * 128)
                T.dma(sp, x1t[b3], self.x1[rows, :], reads=[self._sb(("x1", s))], writes=[B_x1t[b3]])
                stt_ = st[s % 4]
                Bs = B_st[s % 4]
                self.actf(junk, x1t[b3], AF.Square, [B_x1t[b3]], [B_junk, Bs], accum=stt_[:, 0:1])
                self.rstd_from_ss(stt_[:, 2:3], stt_[:, 0:1], stt_[:, 1:2], D, [Bs], [Bs])
                self.ts(dve, hb[b2], x1t[b3], stt_[:, 2:3], None, ALU.mult, None, [B_x1t[b3], Bs], [B_hb[b2]])
                pb, pB = self.bank()
                pbb = pb.bitcast(BF16)
                for k in range(8):
                    self.tr(pbb[:, k * 128:(k + 1) * 128], hb[b2][:, k * 128:(k + 1) * 128], [B_hb[b2]], pB)
                self.cp(dve, hT[:, :, si * 128:(si + 1) * 128], pbb.rearrange("p (a b) -> p a b", a=8, b=128), pB, [B_hT])
            lv2 = int(os.environ.get('P3T2', '9')) if ft >= 1 else 9
            for j in range(22 if lv2 >= 1 else 0):
                pg, pgB = self.bank()
                for k in range(8):
                    self.mm(pg[:, 0:w], wg[:, k, j * 128:(j + 1) * 128], hT[:, k, 0:w], k == 0, k == 7, [B_hT, B_wg], pgB)
                pu, puB = self.bank()
                for k in range(8):
                    self.mm(pu[:, 0:w], wu[:, k, j * 128:(j + 1) * 128], hT[:, k, 0:w], k == 0, k == 7, [B_hT, B_wu], puB)
                sb_ = j % 2
                self.actf(sg[sb_][:, 0:w], pg[:, 0:w], AF.Silu, pgB, [B_sg[sb_]])
                self.tt(dve, aT[:, j, 0:w], sg[sb_][:, 0:w], pu[:, 0:w], ALU.mult, [B_sg[sb_]] + puB, [B_aT])
            pds = []
            for si, s in enumerate(subs if lv2 >= 2 else []):
                pd, pdB = self.bank2()
                for half in range(2):
                    for j in range(22):
                        self.mm(pd[:, half * 512:(half + 1) * 512], aT[:, j, si * 128:(si + 1) * 128], wd[:, j, half * 512:(half + 1) * 512],
                                j == 0, j == 21, [B_aT, B_wd], pdB)
                pds.append((pd, pdB))
            for si, s in enumerate(subs if lv2 >= 3 else []):
                rows = slice(s * 128, (s + 1) * 128)
                b2 = s % 2
                pd, pdB = pds[si]
                T.dma(sp, x1r[b2], self.x1[rows, :], reads=[self._sb(("x1", s))], writes=[B_x1r[b2]])
                stt_ = st[s % 4]
                Bs = B_st[s % 4]
                self.actf(junk2, pd, AF.Square, pdB, [B_junk, Bs], accum=stt_[:, 4:5])
                self.rstd_from_ss(stt_[:, 6:7], stt_[:, 4:5], stt_[:, 5:6], D, [Bs], [Bs])
                self.stt(yo[b2], pd, stt_[:, 6:7], gpost, ALU.mult, ALU.mult, pdB + [Bs, self.B_const], [B_yo[b2]])
                self.tt(pool, yo[b2], yo[b2], x1r[b2], ALU.add, [B_yo[b2], B_x1r[b2]], [B_yo[b2]])
                T.dma(sp, self.y[rows, :], yo[b2], reads=[B_yo[b2]], writes=[self._ob()])


_CACHE = {}


def get_prog(LP, debug=False, stop=9):
    key = (LP, debug, stop)
    if key not in _CACHE:
        _CACHE[key] = Prog(LP, debug, stop)
    return _CACHE[key]


def make_in_maps(LP, inputs):
    x_prompt = np.asarray(inputs["x_prompt"], np.float32)
    x_sample = np.asarray(inputs["x_sample"], np.float32)
    st_conv = np.asarray(inputs["state_conv"], np.float32)[0]
    st_ssm = np.asarray(inputs["state_ssm"], np.float32)[0]
    st_ret = np.asarray(inputs["state_ret"], np.float32)[0]
    params = {k: np.asarray(inputs[k], np.float32)[0] for k in
              ("n_mix_pre", "n_mix_post", "n_ffn_pre", "n_ffn_post", "conv_w", "conv_b", "dt_bias", "a_log", "d_skip",
               "ssm_norm", "ret_norm")}
    ws = {k: np.ascontiguousarray(np.asarray(inputs[k], np.float32)[0]) for k in ("w_in", "w_out", "w_gate", "w_up", "w_down")}
    DS = x_sample.shape[1]
    maps = []
    for c in range(NCORES):
        b, j = c // 4, c % 4
        xs = x_sample[4 * c:4 * c + 4].reshape(4 * DS, D)
        xin = np.concatenate([x_prompt[b, j * LP:(j + 1) * LP], xs], 0)
        xh = np.zeros((128, D), np.float32)
        if j > 0:
            xh[125:128] = x_prompt[b, j * LP - 3:j * LP]
        pos = np.concatenate([np.arange(j * LP, (j + 1) * LP), np.tile(PAST_LEN + np.arange(DS), 4)])
        cs = rope_table(pos)
        cv = st_conv[4 * c:4 * c + 4]
        convst = cv.reshape(4, 3, 12, 128).transpose(3, 2, 0, 1).reshape(128, 144)
        ss = st_ssm[4 * c:4 * c + 4]
        sst = ss.transpose(3, 0, 1, 2).reshape(128, 4 * 1024)
        rs = st_ret[4 * c:4 * c + 4]
        rst = rs.transpose(2, 0, 1, 3).reshape(128, 4 * 1024)
        m = {"xin": np.ascontiguousarray(xin), "xhalo": xh, "cs": cs, "convst": np.ascontiguousarray(convst),
             "sst": np.ascontiguousarray(sst), "rst": np.ascontiguousarray(rst), "cpk": host_consts(c, LP, params)}
        m.update(ws)
        maps.append(m)
    return maps


def assemble(LP, res, nb, DS):
    yp = np.zeros((nb, 4 * LP, D), np.float32)
    ys = np.zeros((4 * NCORES, DS, D), np.float32)
    conv_p = np.zeros((1, nb, 3, CONV), np.float32)
    ssm_p = np.zeros((1, nb, NH, HP, NS), np.float32)
    ret_p = np.zeros((1, nb, RH, RD, RD), np.float32)
    conv_s = np.zeros((1, 4 * NCORES, 3, CONV), np.float32)
    ssm_s = np.zeros((1, 4 * NCORES, NH, HP, NS), np.float32)
    ret_s = np.zeros((1, 4 * NCORES, RH, RD, RD), np.float32)
    for c in range(NCORES):
        r = res[c]
        b, j = c // 4, c % 4
        y = np.asarray(r["y"]).reshape(-1, D)
        yp[b, j * LP:(j + 1) * LP] = y[:LP]
        ys[4 * c:4 * c + 4] = y[LP:].reshape(4, DS, D)
        if j == 3:
            conv_p[0, b] = np.asarray(r["oconv_p"]).reshape(128, 12, 3).transpose(2, 1, 0).reshape(3, CONV)
            ssm_p[0, b] = np.asarray(r["ossm_p"]).reshape(128, NH, HP).transpose(1, 2, 0)
            ret_p[0, b] = np.asarray(r["oret_p"]).reshape(128, RH, RD).transpose(1, 0, 2)
        conv_s[0, 4 * c:4 * c + 4] = np.asarray(r["oconv_s"]).reshape(128, 12, 4, 3).transpose(2, 3, 1, 0).reshape(4, 3, CONV)
        ssm_s[0, 4 * c:4 * c + 4] = np.asarray(r["ossm_s"]).reshape(128, 4, NH, HP).transpose(1, 2, 3, 0)
        ret_s[0, 4 * c:4 * c + 4] = np.asarray(r["oret_s"]).reshape(128, 4, RH, RD).transpose(1, 2, 0, 3)
    return yp, ys, conv_p, ssm_p, ret_p, conv_s, ssm_s, ret_s


def run(inputs, debug=False, stop=9):
    nb, SEQ = inputs["x_prompt"].shape[:2]
    assert nb == 2
    LP = SEQ // 4
    DS = inputs["x_sample"].shape[1]
    prog = get_prog(LP, debug, stop)
    maps = make_in_maps(LP, inputs)
    res = run_bass_kernel_spmd(prog.nc, maps, core_ids=list(range(NCORES)))
    return assemble(LP, res.results, nb, DS), res.results


def kernel(**inputs):
    outs, _ = run(inputs)
    return outs
```
